# Optimizing a Trainium2 kernel written in Bass

```python
import jax, jax.numpy as jnp
from jax import lax
import numpy as np

D_MODEL = 1024
BATCH = 8
SEQ = 4096
DEPTH = 2
DEC_BATCH = 8
DEC_SEQ = 32
PAST_LEN = 4096

CHUNK = 64
MLP_CHUNK = 128
W_A = D_MODEL
HEADS_A = 8
HEAD_DIM_A = W_A // HEADS_A
W_B = D_MODEL
POOL_WINDOWS = (2, 4, 8, 16)
N_POOL_GROUPS = len(POOL_WINDOWS)
POOL_GROUP_DIM = W_B // N_POOL_GROUPS
POOL_STATE = max(POOL_WINDOWS) - 1
D_MIX = W_A + W_B
D_IN_PROJ = 3 * W_A + 2 * W_B
EPS = 1e-6

kernel_name = "hybrid_gmlp_pool_stream_step"


def rmsnorm(x, g):
    xf = x.astype(jnp.float32)
    y = xf * lax.rsqrt(jnp.mean(xf * xf, axis=-1, keepdims=True) + EPS)
    return (y * g.astype(jnp.float32)).astype(x.dtype)


def layernorm(x, g, b):
    xf = x.astype(jnp.float32)
    mu = jnp.mean(xf, axis=-1, keepdims=True)
    xc = xf - mu
    y = xc * lax.rsqrt(jnp.mean(xc * xc, axis=-1, keepdims=True) + EPS)
    return (y * g.astype(jnp.float32) + b.astype(jnp.float32)).astype(x.dtype)


def chunk_mlp(v, w_spatial, b_spatial):
    B, T, _ = v.shape
    pad = (-T) % MLP_CHUNK
    vp = jnp.pad(v, ((0, 0), (0, pad), (0, 0)))
    n = (T + pad) // MLP_CHUNK
    vr = vp.reshape(B, n, MLP_CHUNK, HEADS_A, HEAD_DIM_A)
    blk = jnp.arange(MLP_CHUNK) // CHUNK
    mask = blk[None, :] <= blk[:, None]
    w = jnp.where(mask[None], w_spatial, jnp.zeros_like(w_spatial))
    out = jnp.einsum('hij,bnjhc->bnihc', w, vr) + b_spatial.T[None, None, :, :, None]
    return out.reshape(B, n * MLP_CHUNK, W_A)[:, :T]


def pool_mix(x_ext, pos0, w_pool, pool_scale):
    B, L, _ = x_ext.shape
    T = L - POOL_STATE
    xf = x_ext.astype(jnp.float32)
    cs = jnp.concatenate([jnp.zeros((B, 1, W_B), jnp.float32), jnp.cumsum(xf, axis=1)], axis=1)
    x_new = xf[:, POOL_STATE:]
    pos = jnp.arange(T, dtype=jnp.float32) + pos0
    parts = []
    for g, w in enumerate(POOL_WINDOWS):
        sl = slice(g * POOL_GROUP_DIM, (g + 1) * POOL_GROUP_DIM)
        s = cs[:, POOL_STATE + 1:, sl] - cs[:, POOL_STATE + 1 - w:POOL_STATE + 1 - w + T, sl]
        cnt = jnp.minimum(pos + 1.0, float(w))[None, :, None]
        parts.append(s / cnt - x_new[..., sl])
    d = jnp.stack(parts, axis=2)
    out = jnp.einsum('btgc,gcd->btgd', d, w_pool.astype(jnp.float32)).reshape(B, T, W_B)
    return (out * pool_scale.astype(jnp.float32)).astype(x_ext.dtype)


def mixer_layer(x, pool_past, pos0, norm_g, w_in, ln_v_g, ln_v_b, w_spatial, b_spatial,
                w_pool, pool_scale, w_out):
    h = rmsnorm(x, norm_g)
    proj = jnp.einsum('btd,de->bte', h, w_in)
    u, v, z_a, xb, z_b = jnp.split(proj, [W_A, 2 * W_A, 3 * W_A, 3 * W_A + W_B], axis=-1)
    v_n = layernorm(v, ln_v_g, ln_v_b)
    a = u * chunk_mlp(v_n, w_spatial, b_spatial) * jax.nn.silu(z_a)
    x_ext = jnp.concatenate([pool_past.astype(xb.dtype), xb], axis=1)
    b = pool_mix(x_ext, pos0, w_pool, pool_scale) * jax.nn.silu(z_b)
    y = x + jnp.einsum('bte,ed->btd', jnp.concatenate([a, b], axis=-1), w_out)
    return y, x_ext[:, -POOL_STATE:], v_n


def setup_inputs(seed: int = 0) -> dict:
    key = jax.random.key(seed)
    ks = jax.random.split(key, 16)
    f32 = jnp.float32
    x_prompt = jax.random.normal(ks[0], (BATCH, SEQ, D_MODEL), f32)
    x_sample = jax.random.normal(ks[1], (DEC_BATCH, DEC_SEQ, D_MODEL), f32)
    state_pool = jax.random.normal(ks[2], (DEPTH, DEC_BATCH, POOL_STATE, W_B), f32)
    norm_g = 1.0 + 0.05 * jax.random.normal(ks[3], (DEPTH, D_MODEL), f32)
    w_in = jax.random.normal(ks[4], (DEPTH, D_MODEL, D_IN_PROJ), f32) * D_MODEL ** -0.5
    ln_v_g = 1.0 + 0.05 * jax.random.normal(ks[5], (DEPTH, W_A), f32)
    ln_v_b = 0.02 * jax.random.normal(ks[6], (DEPTH, W_A), f32)
    w_spatial = jax.random.normal(ks[7], (DEPTH, HEADS_A, MLP_CHUNK, MLP_CHUNK), f32) * MLP_CHUNK ** -0.5
    b_spatial = 1.0 + 0.1 * jax.random.normal(ks[8], (DEPTH, HEADS_A, MLP_CHUNK), f32)
    w_pool = jax.random.normal(ks[9], (DEPTH, N_POOL_GROUPS, POOL_GROUP_DIM, POOL_GROUP_DIM), f32) * POOL_GROUP_DIM ** -0.5
    pool_scale = 1.0 + 0.1 * jax.random.normal(ks[10], (DEPTH, W_B), f32)
    w_out = jax.random.normal(ks[11], (DEPTH, D_MIX, D_MODEL), f32) * D_MIX ** -0.5
    final_g = 1.0 + 0.05 * jax.random.normal(ks[12], (D_MODEL,), f32)
    return {"x_prompt": x_prompt, "x_sample": x_sample, "state_pool": state_pool,
            "norm_g": norm_g, "w_in": w_in, "ln_v_g": ln_v_g, "ln_v_b": ln_v_b,
            "w_spatial": w_spatial, "b_spatial": b_spatial, "w_pool": w_pool,
            "pool_scale": pool_scale, "w_out": w_out, "final_g": final_g}


def reference(x_prompt, x_sample, state_pool, norm_g, w_in, ln_v_g, ln_v_b, w_spatial,
              b_spatial, w_pool, pool_scale, w_out, final_g):
    xp = x_prompt
    xs = x_sample
    pool_p, pool_s, v_s = [], [], []
    zero_past = jnp.zeros((x_prompt.shape[0], POOL_STATE, W_B), x_prompt.dtype)
    for l in range(DEPTH):
        params = (norm_g[l], w_in[l], ln_v_g[l], ln_v_b[l], w_spatial[l], b_spatial[l],
                  w_pool[l], pool_scale[l], w_out[l])
        xp, new_pool_p, _ = mixer_layer(xp, zero_past, 0, *params)
        xs, new_pool_s, v_rows = mixer_layer(xs, state_pool[l], PAST_LEN, *params)
        pool_p.append(new_pool_p)
        pool_s.append(new_pool_s)
        v_s.append(v_rows)
    y_prompt = rmsnorm(xp, final_g)
    y_sample = rmsnorm(xs, final_g)
    new_state_pool_prompt = jnp.stack(pool_p, axis=0)
    new_state_pool_sample = jnp.stack(pool_s, axis=0)
    new_chunk_v_sample = jnp.stack(v_s, axis=0)
    return (y_prompt, y_sample, new_state_pool_prompt, new_state_pool_sample, new_chunk_v_sample)
```

```python
import contextlib
import os
DBG = os.environ.get('KDBG', '')
import numpy as np
import concourse.bass as bass
import concourse.mybir as mybir
from concourse.bass_utils import run_bass_kernel_spmd

F32 = mybir.dt.float32
BF16 = mybir.dt.bfloat16
ALU = mybir.AluOpType
AF = mybir.ActivationFunctionType

ENGS = ("pe", "act", "dve", "pool", "sp")
D = 1024
DEPTH = 2
NH = 8
PST = 15
EPS = 1e-6
WINS = (2, 4, 8, 16)
WSLOTS = 6


class Prog:
    def __init__(self, nc):
        self.nc = nc
        self.ops = {e: [] for e in ENGS}
        self.cnt = {}
        self.known = {e: {} for e in ENGS}
        self.res = {}

    def _collect(self, reads, writes):
        deps = {}

        def add(sig):
            if sig is None:
                return
            k, v = sig
            if deps.get(k, 0) < v:
                deps[k] = v

        for r in reads:
            st = self.res.get(r)
            if st is not None:
                add(st[0])
        for w in writes:
            st = self.res.get(w)
            if st is not None:
                add(st[0])
                for k, v in st[1].items():
                    add((k, v))
        return deps

    def _record(self, sig, reads, writes):
        for r in reads:
            st = self.res.setdefault(r, [None, {}])
            if st[1].get(sig[0], 0) < sig[1]:
                st[1][sig[0]] = sig[1]
        for w in writes:
            self.res[w] = [sig, {}]

    def _waits(self, eng, deps):
        out = []
        kn = self.known[eng]
        for k, v in deps.items():
            if kn.get(k, 0) < v:
                kn[k] = v
                out.append((k, v))
        return out

    def op(self, eng, fn, reads=(), writes=()):
        deps = self._collect(reads, writes)
        waits = self._waits(eng, deps)
        self.cnt[eng] = self.cnt.get(eng, 0) + 1
        sig = (eng, self.cnt[eng])
        self.ops[eng].append((waits, fn, (eng, 1)))
        self._record(sig, reads, writes)
        return sig

    def dma(self, eng, semkey, fn, reads=(), writes=()):
        deps = self._collect(reads, writes)
        waits = self._waits(eng, deps)
        self.cnt[semkey] = self.cnt.get(semkey, 0) + 16
        sig = (semkey, self.cnt[semkey])
        self.ops[eng].append((waits, fn, (semkey, 16)))
        self._record(sig, reads, writes)
        return sig

    def dma_group(self, eng, semkey, items):
        final = self.cnt.get(semkey, 0) + 16 * len(items)
        for fn, reads, writes in items:
            deps = self._collect(reads, writes)
            waits = self._waits(eng, deps)
            self.ops[eng].append((waits, fn, (semkey, 16)))
        self.cnt[semkey] = final
        for fn, reads, writes in items:
            self._record((semkey, final), reads, writes)

    def pe_group(self, mms, writes):
        eng = "pe"
        sig = (eng, self.cnt.get(eng, 0) + 1)
        n = len(mms)
        for i, (fn, reads) in enumerate(mms):
            deps = self._collect(reads, writes if i == 0 else ())
            waits = self._waits(eng, deps)
            self.ops[eng].append((waits, fn, (eng, 1) if i == n - 1 else None))
        self.cnt[eng] = sig[1]
        allreads = set()
        for _, reads in mms:
            allreads.update(reads)
        self._record(sig, tuple(allreads), writes)
        return sig

    def wait_all(self, eng, semkeys):
        deps = {k: self.cnt[k] for k in semkeys if self.cnt.get(k, 0) > 0}
        waits = self._waits(eng, deps)
        if waits:
            self.ops[eng].append((waits, None, None))

    def emit(self, stack):
        nc = self.nc
        sems = {}
        for k in self.cnt:
            sems[k] = stack.enter_context(nc.semaphore("s_" + str(k)))
        block = stack.enter_context(nc.Block())
        ops = self.ops

        def run(engine, lst):
            for waits, fn, inc in lst:
                for k, v in waits:
                    engine.wait_ge(sems[k], v)
                if fn is None:
                    continue
                ins = fn(engine)
                if inc is not None:
                    ins.then_inc(sems[inc[0]], inc[1])

        @block.tensor
        def _(e):
            run(e, ops["pe"])

        @block.scalar
        def _(e):
            run(e, ops["act"])

        @block.vector
        def _(e):
            run(e, ops["dve"])

        @block.gpsimd
        def _(e):
            run(e, ops["pool"])

        @block.sync
        def _(e):
            run(e, ops["sp"])


class Seg:
    pass


class _Stop(Exception):
    pass


def _stage(n):
    if ("stage=%d," % n) in DBG + ",":
        raise _Stop()


PIECES = ("v", "xb", "za", "zb", "u", "woa", "wob")
IN_COL = {"u": 0, "v": 1024, "za": 2048, "xb": 3072, "zb": 4096}


def build_program(SEQ):
    NU = SEQ // 512
    nc = bass.Bass("TRN2", target_bir_lowering=False)
    if "noprecook" in DBG:
        nc.dge_precook = False
    dt_in = lambda name, shape: nc.dram_tensor(name, shape, F32, kind="ExternalInput").ap()
    dt_out = lambda name, shape: nc.dram_tensor(name, shape, F32, kind="ExternalOutput").ap()
    xp_d = dt_in("xp", [SEQ, D])
    xs_d = dt_in("xs", [32, D])
    sp_d = dt_in("sp", [128, DEPTH, 8, PST])
    identf_d = dt_in("identf", [128, 128])
    w_in_d = dt_in("w_in", [DEPTH, D, 5 * D])
    w_out_d = dt_in("w_out", [DEPTH, 2 * D, D])
    wsT_d = dt_in("wsT", [128, DEPTH, NH, 128])
    wpool_d = dt_in("wpool", [DEPTH, 4, 256, 256])
    ng_d = dt_in("ng", [128, DEPTH, 8])
    psc_d = dt_in("psc", [128, DEPTH, 8])
    lng_d = dt_in("lng", [128, DEPTH, 8])
    lnb_d = dt_in("lnb", [128, DEPTH, 8])
    lngr_d = dt_in("lngr", [DEPTH, D])
    lnbr_d = dt_in("lnbr", [DEPTH, D])
    bsp_d = dt_in("bsp", [DEPTH * NH * 128])
    fg_d = dt_in("fg", [D])
    ident_d = dt_in("ident", [128, 128])
    invc_d = dt_in("invc", [128, 4, PST])
    yp_d = dt_out("yp", [SEQ, D])
    ys_d = dt_out("ys", [32, D])
    pp_d = dt_out("pp", [DEPTH, PST, D])
    pso_d = dt_out("pso", [DEPTH, PST, D])
    vs_d = dt_out("vs", [DEPTH, 32, D])
    scr = {}
    for l in range(DEPTH):
        for pc in PIECES:
            for hf in range(2):
                scr[(l, pc, hf)] = nc.dram_tensor("scr_%d_%s_%d" % (l, pc, hf), [128, 8, 512], BF16).ap()

    st = contextlib.ExitStack()
    with st:
        sb = lambda name, shape, dt: st.enter_context(nc.sbuf_tensor("sb_" + name, shape, dt))
        ps = st.enter_context(nc.psum_tensor("ps", [128, 4096], F32))
        wring = sb("wring", [128, WSLOTS, 8, 512], BF16)
        wsT = sb("wsT_b", [128, DEPTH, NH, 128], BF16)
        wpool = sb("wpool_b", [128, DEPTH, 4, 2, 256], BF16)
        ng = sb("ng", [128, DEPTH, 8], F32)
        psc = sb("psc", [128, DEPTH, 8], F32)
        lng = sb("lng", [128, DEPTH, 8], F32)
        lnb = sb("lnb", [128, DEPTH, 8], F32)
        biasf = sb("biasf", [128, DEPTH, NH, 128], F32)
        fgB = sb("fgB", [128, D], F32)
        ident = sb("ident_b", [128, 128], BF16)
        ones = sb("ones_b", [128, 128], BF16)
        invc = sb("invc", [128, 4, PST], F32)
        hist = sb("hist", [128, DEPTH, 8, PST], F32)
        hist_s = sb("hist_s", [128, DEPTH, 8, PST], F32)
        identf = sb("identf", [128, 128], F32)
        xbuf = sb("xbuf", [128, 2, 4, D], F32)
        xsam = sb("xsam", [32, 1, D], F32)

        def mkseg(name, T, R, NT):
            s = Seg()
            s.name, s.T, s.R, s.NT = name, T, R, NT
            s.L = PST + T
            s.xsvn = sb(name + "_xsvn", [R, NT, D], BF16)
            s.hT = sb(name + "_hT", [128, 8, T], BF16)
            s.xbe = sb(name + "_xbe", [128, 8, s.L], F32)
            s.pa = sb(name + "_pa", [128, 2, s.L], F32)
            s.pb = sb(name + "_pb", [128, 2, s.L], F32)
            s.dT = sb(name + "_dT", [128, 8, T], BF16)
            s.sza = sb(name + "_sza", [128, 8, T], BF16)
            s.szb = sb(name + "_szb", [128, 8, T], BF16)
            s.t1 = sb(name + "_t1", [128, 2, T], F32)
            s.mixT = sb(name + "_mixT", [128, 16, T], BF16)
            s.ms = sb(name + "_ms", [R, 3 * NT], F32)
            s.rt = sb(name + "_rt", [R, 3 * NT], F32)
            s.rstd = sb(name + "_rstd", [R, 3 * NT], F32)
            s.bst = sb(name + "_bst", [R, NT, 12], F32)
            s.mv = sb(name + "_mv", [R, NT, 2], F32)
            s.fix = sb(name + "_fix", [128, PST], F32)
            return s

        segP = mkseg("P", 512, 128, 4)
        segS = mkseg("S", 32, 32, 1)
        vnf = sb("S_vnf", [32, 512], F32)

        P = Prog(nc)
        nbank = [0]

        def newbank():
            b = nbank[0] % 8
            nbank[0] += 1
            return b, ps[:, b * 512:(b + 1) * 512], ("bank", b)

        cast_idx = [0]

        def cast_piece(l, pc):
            slot_key = ("castslot", cast_idx[0] % 2)
            cast_idx[0] += 1
            items = []
            for hf in range(2):
                if pc in IN_COL:
                    c0 = IN_COL[pc] + hf * 512
                    src = w_in_d[l, :, c0:c0 + 512].rearrange("(c p) e -> p c e", p=128)
                else:
                    r0 = 0 if pc == "woa" else 1024
                    src = w_out_d[l, r0:r0 + 1024, hf * 512:(hf + 1) * 512].rearrange("(c p) e -> p c e", p=128)
                dst = scr[(l, pc, hf)]
                items.append(((lambda e, dst=dst, src=src: e.dma_start(out=dst, in_=src)), (), (("scr", l, pc, hf), slot_key)))
            P.dma_group("pool", ("cast", l, pc), items)
        P.dma_group("pool", "c_cast", [
            (lambda e: e.dma_start(out=ident[:], in_=ident_d), (), ("ident",)),
            (lambda e: e.dma_start(out=wsT[:], in_=wsT_d), (), ("wsT",)),
            (lambda e: e.dma_start(out=wpool[:, 0], in_=wpool_d[0].rearrange("g (c p) d -> p g c d", p=128)), (), (("wpool", 0),)),
            (lambda e: e.dma_start(out=wpool[:, 1], in_=wpool_d[1].rearrange("g (c p) d -> p g c d", p=128)), (), (("wpool", 1),)),
        ])
        for l in range(DEPTH):
            for pc in PIECES:
                if "nocast" not in DBG:
                    cast_piece(l, pc)
        P.dma_group("sp", "c_ws", [
            (lambda e: e.dma_start(out=ng[:], in_=ng_d), (), ("ng",)),
            (lambda e: e.dma_start(out=psc[:], in_=psc_d), (), ("psc",)),
            (lambda e: e.dma_start(out=lng[:], in_=lng_d), (), ("lng",)),
            (lambda e: e.dma_start(out=lnb[:], in_=lnb_d), (), ("lnb",)),
            (lambda e: e.dma_start(out=invc[:], in_=invc_d), (), ("invc",)),
            (lambda e: e.dma_start(out=fgB[:], in_=fg_d.partition_broadcast(128)), (), ("fgB",)),
            (lambda e: e.dma_start(out=xsam[:, 0, :], in_=xs_d), (), ((("xs",), 0),)),
            (lambda e: e.dma_start(out=hist_s[:], in_=sp_d), (), ("hist_s",)),
            (lambda e: e.dma_start(out=identf[:], in_=identf_d), (), ("identf",)),
        ])
        if "stop0" in DBG:
            P.wait_all("sp", [k for k in P.cnt if k not in ENGS]); P.emit(st); return nc
        P.op("dve", lambda e: e.memset(ones[:], 1.0), writes=("ones",))
        P.op("dve", lambda e: e.memset(wsT[64:128, :, :, 0:64], 0.0), reads=(), writes=("wsT",))
        bspB = segP.t1[:].rearrange("p a t -> p (a t)")
        t1keys = (("t1", "P", 0), ("t1", "P", 1))
        for l in range(DEPTH):
            P.dma("sp", "c_bsp", (lambda e, l=l: e.dma_start(
                out=bspB, in_=bsp_d[l * NH * 128:(l + 1) * NH * 128].partition_broadcast(128))), writes=t1keys)
            for hh in range(2):
                b, bk, bkey = newbank()
                P.pe_group([((lambda e, bk=bk, l=l, hh=hh: e.matmul(
                    bk, lhsT=ones[:], rhs=wsT[:, l, hh * 4:(hh + 1) * 4, :], start=True, stop=True)),
                    ("ones", "wsT"))], writes=(bkey,))
                for h4 in range(4):
                    h = hh * 4 + h4
                    o0 = h * 128
                    P.op("dve", (lambda e, bk=bk, l=l, h=h, h4=h4, o0=o0: e.scalar_tensor_tensor(
                        out=biasf[:, l, h, :], in0=bk[:, h4 * 128:(h4 + 1) * 128], scalar=lnb[:, l, h:h + 1],
                        in1=bspB[:, o0:o0 + 128], op0=ALU.mult, op1=ALU.add)),
                        reads=(bkey, "lnb") + t1keys, writes=(("biasf", l, h),))

        wcount = [0]

        def load_piece(l, pc, hf):
            k = wcount[0] % WSLOTS
            wcount[0] += 1
            src = scr[(l, pc, hf)]
            P.dma("sp", ("wl", k), (lambda e, k=k, src=src: e.dma_start(out=wring[:, k], in_=src)),
                  reads=(("scr", l, pc, hf),), writes=(("wslot", k),))
            return k

        def xview(seg, slot):
            return xbuf[:, slot] if seg.R == 128 else xsam[:]

        def pre_tile(seg, xv, xkey, l, i, col):
            R = seg.R
            n = seg.name
            P.op("act", (lambda e: e.activation(out=seg.xsvn[:, i, :], in_=xv[:, i, :], func=AF.Square,
                                                scale=1.0 / 32.0, accum_out=seg.ms[:, col:col + 1])),
                 reads=((xkey, i),), writes=(("xsvn", n, i), ("ms", n, col)))
            P.op("act", (lambda e: e.activation(out=seg.rt[:, col:col + 1], in_=seg.ms[:, col:col + 1], func=AF.Sqrt,
                                                bias=EPS, scale=1.0)),
                 reads=(("ms", n, col),), writes=(("rt", n, col),))
            P.op("dve", (lambda e: e.reciprocal(out=seg.rstd[:, col:col + 1], in_=seg.rt[:, col:col + 1])),
                 reads=(("rt", n, col),), writes=(("rstd", n, col),))

        def pre_xs(seg, xv, xkey, i, col):
            n = seg.name
            P.op("act", (lambda e: e.activation(out=seg.xsvn[:, i, :], in_=xv[:, i, :], func=AF.Identity,
                                                scale=seg.rstd[:, col:col + 1])),
                 reads=((xkey, i), ("rstd", n, col)), writes=(("xsvn", n, i),))

        def transposes(seg, l, tiles, evac_i=[0]):
            R, n = seg.R, seg.name
            nt = len(tiles)
            W = nt * 128 if R == 128 else 32
            nj = max(1, min(8, 1024 // W))
            for j0 in range(0, 8, nj):
                b, bk, bkey = newbank()
                psb = bk.bitcast(BF16)
                mms = []
                for jj in range(nj):
                    j = j0 + jj
                    for ti, i in enumerate(tiles):
                        c0 = jj * W + ti * 128
                        mms.append(((lambda e, psb=psb, c0=c0, i=i, j=j: e.transpose(
                            out=psb[:, c0:c0 + R], in_=seg.xsvn[:, i, j * 128:(j + 1) * 128], identity=ident[0:R, 0:R])),
                            (("xsvn", n, i), "ident")))
                P.pe_group(mms, writes=(bkey,))
                for jj in range(nj):
                    j = j0 + jj
                    t0 = tiles[0] * 128
                    src = psb[:, jj * W:(jj + 1) * W]
                    dst = seg.hT[:, j, t0:t0 + W]
                    wr = tuple(("hT", n, j, i) for i in tiles)
                    if True:
                        P.op("dve", (lambda e, src=src, dst=dst, j=j: e.tensor_scalar(
                            out=dst, in0=src, scalar1=ng[:, l, j:j + 1], scalar2=None, op0=ALU.mult)),
                            reads=(bkey, "ng"), writes=wr)
                    else:
                        P.op("act", (lambda e, src=src, dst=dst, j=j: e.activation(
                            out=dst, in_=src, func=AF.Identity, scale=ng[:, l, j:j + 1])),
                            reads=(bkey, "ng"), writes=wr)
                    evac_i[0] += 1

        def hT_reads(seg, dc):
            return tuple(("hT", seg.name, dc, i) for i in range(seg.NT))

        def v_piece(seg, l, slots, i, sample_out):
            R, n = seg.R, seg.name
            banks = []
            for hf in range(2):
                b, bk, bkey = newbank()
                k = slots[hf]
                mms = []
                for dc in range(8):
                    mms.append(((lambda e, bk=bk, dc=dc, k=k: e.matmul(
                        bk[0:R, :], lhsT=seg.hT[:, dc, i * 128:i * 128 + R], rhs=wring[:, k, dc, :],
                        start=(dc == 0), stop=(dc == 7))),
                        (("hT", n, dc, i), ("wslot", k))))
                P.pe_group(mms, writes=(bkey,))
                banks.append((bk, bkey))
                P.op("dve", (lambda e, bk=bk, hf=hf: e.bn_stats(out=seg.bst[:, i, hf * 6:(hf + 1) * 6], in_=bk[0:R, :])),
                     reads=(bkey,), writes=(("bst", n, i, hf),))
            P.op("dve", (lambda e: e.bn_aggr(out=seg.mv[:, i, :], in_=seg.bst[:, i, :])),
                 reads=(("bst", n, i, 0), ("bst", n, i, 1)), writes=(("mv", n, i),))
            c = seg.NT + i
            P.op("act", (lambda e: e.activation(out=seg.rt[:, c:c + 1], in_=seg.mv[:, i, 1:2], func=AF.Sqrt, bias=EPS, scale=1.0)),
                 reads=(("mv", n, i),), writes=(("rt", n, c),))
            P.op("dve", (lambda e: e.reciprocal(out=seg.rstd[:, c:c + 1], in_=seg.rt[:, c:c + 1])),
                 reads=(("rt", n, c),), writes=(("rstd", n, c),))
            if sample_out:
                P.dma("pool", "s_lg", lambda e: e.dma_start(out=segP.pa[0:32, :, 0:512], in_=lngr_d[l].rearrange("(a c) -> a c", a=2).partition_broadcast(32)),
                      writes=(("pa", "P"),))
                P.dma("pool", "s_lb", lambda e: e.dma_start(out=segP.pb[0:32, :, 0:512], in_=lnbr_d[l].rearrange("(a c) -> a c", a=2).partition_broadcast(32)),
                      writes=(("pb", "P"),))
            for hf in range(2):
                bk, bkey = banks[hf]
                if sample_out:
                    P.op("dve", (lambda e, bk=bk, hf=hf: e.tensor_scalar(
                        out=vnf[:], in0=bk[0:R, :], scalar1=seg.mv[:, i, 0:1],
                        scalar2=seg.rstd[:, c:c + 1], op0=ALU.subtract, op1=ALU.mult)),
                        reads=(bkey, ("mv", n, i), ("rstd", n, c)), writes=("vnf",))
                    P.op("dve", (lambda e, hf=hf: e.tensor_tensor(out=vnf[:], in0=vnf[:], in1=segP.pa[0:32, hf, 0:512], op=ALU.mult)),
                         reads=("vnf", ("pa", "P")), writes=("vnf",))
                    P.op("dve", (lambda e, hf=hf: e.tensor_tensor(out=vnf[:], in0=vnf[:], in1=segP.pb[0:32, hf, 0:512], op=ALU.add)),
                         reads=("vnf", ("pb", "P")), writes=("vnf",))
                    P.dma("pool", "st_vs", (lambda e, hf=hf: e.dma_start(out=vs_d[l, :, hf * 512:(hf + 1) * 512], in_=vnf[:])),
                          reads=("vnf",))
                P.op("dve", (lambda e, bk=bk, hf=hf: e.tensor_scalar(
                    out=seg.xsvn[:, i, hf * 512:(hf + 1) * 512], in0=bk[0:R, :], scalar1=seg.mv[:, i, 0:1],
                    scalar2=seg.rstd[:, c:c + 1], op0=ALU.subtract, op1=ALU.mult)),
                    reads=(bkey, ("mv", n, i), ("rstd", n, c)), writes=(("xsvn", n, i),))

        def fm_group(seg, k, eo):
            n, T = seg.name, seg.T
            b, bk, bkey = newbank()
            mms = []
            for dc in range(8):
                mms.append(((lambda e, bk=bk, dc=dc: e.matmul(
                    bk[:, 0:T], lhsT=wring[:, k, dc, eo * 128:(eo + 1) * 128], rhs=seg.hT[:, dc, 0:T],
                    start=(dc == 0), stop=(dc == 7))),
                    hT_reads(seg, dc) + (("wslot", k),)))
            P.pe_group(mms, writes=(bkey,))
            return bk, bkey

        def pooling(seg, l, first):
            n, T, L = seg.name, seg.T, seg.L
            thunks = []

            def emit(*a, **k):
                thunks.append(lambda: P.op(*a, **k))
            for g in range(4):
                w = WINS[g]
                src = seg.xbe[:, 2 * g:2 * g + 2, :]
                skeys = (("xbe", n, 2 * g), ("xbe", n, 2 * g + 1))
                bufs = [(seg.pa, ("pa", n)), (seg.pb, ("pb", n))]
                cur, ckeys = src, skeys
                sh = 1
                for lev in range(g + 1):
                    dstb, dkey = bufs[lev % 2]
                    lo = 2 * sh - 1
                    emit("dve", (lambda e, dstb=dstb, cur=cur, lo=lo, sh=sh: e.tensor_tensor(
                        out=dstb[:, :, lo:L], in0=cur[:, :, lo:L], in1=cur[:, :, lo - sh:L - sh], op=ALU.add)),
                        reads=ckeys, writes=(dkey,))
                    cur, ckeys = dstb, (dkey,)
                    sh *= 2
                emit("dve", (lambda e, cur=cur, src=src, g=g, w=w: e.scalar_tensor_tensor(
                    out=seg.dT[:, 2 * g:2 * g + 2, :], in0=cur[:, :, PST:L], scalar=1.0 / w, in1=src[:, :, PST:L],
                    op0=ALU.mult, op1=ALU.subtract)),
                    reads=ckeys + skeys, writes=(("dT", n, 2 * g), ("dT", n, 2 * g + 1)))
                if first:
                    for cb in (2 * g, 2 * g + 1):
                        c2 = cb - 2 * g
                        emit("dve", (lambda e, cur=cur, c2=c2, g=g: e.tensor_tensor(
                            out=seg.fix[:], in0=cur[:, c2, PST:2 * PST], in1=invc[:, g, :], op=ALU.mult)),
                            reads=ckeys + ("invc",), writes=(("fix", n),))
                        emit("dve", (lambda e, cb=cb: e.tensor_tensor(
                            out=seg.dT[:, cb, 0:PST], in0=seg.fix[:], in1=seg.xbe[:, cb, PST:2 * PST], op=ALU.subtract)),
                            reads=(("fix", n), ("xbe", n, cb)), writes=(("dT", n, cb),))
            return thunks

        def spatial_head(seg, l, h):
            n, T, R, NT = seg.name, seg.T, seg.R, seg.NT
            if True:
                b, bk, bkey = newbank()
                mms = []
                for i in range(NT):
                    mms.append(((lambda e, bk=bk, i=i, h=h: e.matmul(
                        bk[:, i * 128:i * 128 + R], lhsT=seg.xsvn[:, i, h * 128:(h + 1) * 128], rhs=wsT[0:R, l, h, 0:R],
                        start=True, stop=True)),
                        (("xsvn", n, i), "wsT")))
                P.pe_group(mms, writes=(bkey,))
                sl = h % 2
                if R == 128:
                    in0 = bk[:, 0:T].rearrange("p (a i) -> p a i", i=128)
                    o = seg.t1[:, sl, :].rearrange("p (a i) -> p a i", i=128)
                    bf = biasf[:, l, h, :]
                    in1 = bass.AP(bf.tensor, bf.offset, [list(bf.ap[0]), [0, NT], list(bf.ap[-1])])
                else:
                    in0 = bk[:, 0:T]
                    o = seg.t1[:, sl, :]
                    in1 = biasf[:, l, h, 0:R]
                P.op("dve", (lambda e, in0=in0, o=o, in1=in1, h=h: e.scalar_tensor_tensor(
                    out=o, in0=in0, scalar=lng[:, l, h:h + 1], in1=in1, op0=ALU.mult, op1=ALU.add)),
                    reads=(bkey, "lng", ("biasf", l, h)), writes=(("t1", n, sl),))
                P.op("dve", (lambda e, sl=sl, h=h: e.tensor_tensor(
                    out=seg.sza[:, h, :], in0=seg.t1[:, sl, :], in1=seg.sza[:, h, :], op=ALU.mult)),
                    reads=(("t1", n, sl), ("sza", n, h)), writes=(("sza", n, h),))

        def pool_linear_blk(seg, l, fb0):
            n, T = seg.name, seg.T
            for g in (fb0 // 2,):
                for db in (fb0 % 2,):
                    b, bk, bkey = newbank()
                    mms = []
                    for cc in range(2):
                        mms.append(((lambda e, bk=bk, g=g, cc=cc, db=db: e.matmul(
                            bk[:, 0:T], lhsT=wpool[:, l, g, cc, db * 128:(db + 1) * 128], rhs=seg.dT[:, 2 * g + cc, :],
                            start=(cc == 0), stop=(cc == 1))),
                            (("dT", n, 2 * g + cc), ("wpool", l))))
                    P.pe_group(mms, writes=(bkey,))
                    fb = 2 * g + db
                    P.op("dve", (lambda e, bk=bk, fb=fb: e.scalar_tensor_tensor(
                        out=seg.mixT[:, 8 + fb, :], in0=bk[:, 0:T], scalar=psc[:, l, fb:fb + 1], in1=seg.szb[:, fb, :],
                        op0=ALU.mult, op1=ALU.mult)),
                        reads=(bkey, "psc", ("szb", n, fb)), writes=(("mixT", n, 8 + fb),))

        def out_tile(seg, l, xv, xkey, slots4, i):
            n, R = seg.name, seg.R
            for hf in range(2):
                b, bk, bkey = newbank()
                mms = []
                order = list(range(8, 16)) + list(range(8))
                for q, ec in enumerate(order):
                    k = slots4[(0 if ec < 8 else 1, hf)]
                    mms.append(((lambda e, bk=bk, ec=ec, k=k, q=q: e.matmul(
                        bk[0:R, :], lhsT=seg.mixT[:, ec, i * 128:i * 128 + R], rhs=wring[:, k, ec % 8, :],
                        start=(q == 0), stop=(q == 15))),
                        (("mixT", n, ec), ("wslot", k))))
                P.pe_group(mms, writes=(bkey,))
                P.op("dve", (lambda e, bk=bk, hf=hf: e.tensor_tensor(
                    out=xv[:, i, hf * 512:(hf + 1) * 512], in0=bk[0:R, :], in1=xv[:, i, hf * 512:(hf + 1) * 512], op=ALU.add)),
                    reads=(bkey, (xkey, i)), writes=((xkey, i),))

        def final_tile(seg, xv, xkey, i, col):
            n, R = seg.name, seg.R
            P.op("dve", (lambda e: e.scalar_tensor_tensor(
                out=xv[:, i, :], in0=xv[:, i, :], scalar=seg.rstd[:, col:col + 1], in1=fgB[0:R, :],
                op0=ALU.mult, op1=ALU.mult)),
                reads=((xkey, i), ("rstd", n, col), "fgB"), writes=((xkey, i),))

        def load_x(u):
            slot = u % 2
            if "x1s0" in DBG: slot = 0
            if "x1r0" in DBG: u = 0
            P.dma("sp", ("xl", slot), (lambda e: e.dma_start(
                out=xbuf[:, slot], in_=xp_d[u * 512:(u + 1) * 512, :].rearrange("(a p) d -> p a d", p=128))),
                writes=tuple((("x", slot), i) for i in range(4)))

        load_x(0)
        if "extra" in DBG:
            for _ in range(6):
                P.dma("sp", "c_extra", lambda e: e.dma_start(out=ng[:], in_=ng_d), writes=("ng",))
        out_sems = ["st_vs"]

        def stats_junk(seg, i):
            v = seg.dT[:].rearrange("p (a b) t -> p a (b t)", b=2) if seg.R == 128 else None
            return v

        def final_stats(seg, xv, xkey, i, col):
            n = seg.name
            if seg.R == 128:
                junk = seg.dT[:, 2 * i:2 * i + 2, :]
                src = xv[:, i, :].rearrange("p (b t) -> p b t", b=2)
                jk = (("dT", n, 2 * i), ("dT", n, 2 * i + 1))
            else:
                junk = seg.xsvn[:, 0, :]
                src = xv[:, i, :]
                jk = (("xsvn", n, 0),)
            P.op("act", (lambda e: e.activation(out=junk, in_=src, func=AF.Square,
                                                scale=1.0 / 32.0, accum_out=seg.ms[:, col:col + 1])),
                 reads=((xkey, i),), writes=jk + (("ms", n, col),))
            P.op("act", (lambda e: e.activation(out=seg.rt[:, col:col + 1], in_=seg.ms[:, col:col + 1], func=AF.Sqrt,
                                                bias=EPS, scale=1.0)),
                 reads=(("ms", n, col),), writes=(("rt", n, col),))
            P.op("dve", (lambda e: e.reciprocal(out=seg.rstd[:, col:col + 1], in_=seg.rt[:, col:col + 1])),
                 reads=(("rt", n, col),), writes=(("rstd", n, col),))

        def pool_rows_out(seg, l, dst_d, semkey):
            n, T = seg.name, seg.T
            stage = segP.t1[:].rearrange("p a t -> p (a t)")
            for hb in range(2):
                b, bk, bkey = newbank()
                mms = []
                for c4 in range(4):
                    cb = hb * 4 + c4
                    mms.append(((lambda e, bk=bk, c4=c4, cb=cb: e.transpose(
                        out=bk[0:PST, c4 * 128:(c4 + 1) * 128], in_=seg.xbe[:, cb, T:T + PST], identity=identf[:])),
                        (("xbe", n, cb), "identf")))
                P.pe_group(mms, writes=(bkey,))
                P.op("act", (lambda e, bk=bk, hb=hb: e.activation(
                    out=stage[0:PST, hb * 512:(hb + 1) * 512], in_=bk[0:PST, :], func=AF.Copy)),
                    reads=(bkey,), writes=(("t1", "P", hb),))
            P.dma("pool", semkey, (lambda e: e.dma_start(out=dst_d[l], in_=stage[0:PST, :])),
                  reads=(("t1", "P", 0), ("t1", "P", 1)))
            if semkey not in out_sems:
                out_sems.append(semkey)

        def fm_piece(l, pc, segs, consume):
            for hf in range(2):
                k = load_piece(l, pc, hf)
                for seg, xv, xkey in segs:
                    for eo in range(4):
                        bk, bkey = fm_group(seg, k, eo)
                        consume(seg, hf * 4 + eo, bk, bkey)

        def main_loop():
          pre_tile(segS, xsam[:], ("xs",), 0, 0, 0)
          pre_xs(segS, xsam[:], ("xs",), 0, 0)
          transposes(segS, 0, [0])
          for i in range(4):
              pre_tile(segP, xbuf[:, 0], ("x", 0), 0, i, i)
              pre_xs(segP, xbuf[:, 0], ("x", 0), i, i)
          transposes(segP, 0, [0, 1, 2, 3])
          pending = [None]
          for u in range(NU if "stop" not in DBG else 0):
              slot = u % 2
              segs = [(segP, xbuf[:, slot], ("x", slot))]
              if u == NU - 1:
                  segs.append((segS, xsam[:], ("xs",)))
              for l in range(DEPTH):
                  _stage(1)
                  for seg, xv, xkey in segs:
                      n = seg.name
                      wr = tuple(("xbe", n, cb) for cb in range(8))
                      if seg.R == 128:
                          if u == 0:
                              P.op("dve", (lambda e, seg=seg: e.memset(seg.xbe[:, :, 0:PST], 0.0)), writes=wr)
                          else:
                              P.op("act", (lambda e, seg=seg, l=l: e.activation(out=seg.xbe[:, :, 0:PST], in_=hist[:, l], func=AF.Copy)),
                                   reads=(("hist", l),), writes=wr)
                      else:
                          P.op("act", (lambda e, seg=seg, l=l: e.activation(out=seg.xbe[:, :, 0:PST], in_=hist_s[:, l], func=AF.Copy)),
                               reads=("hist_s",), writes=wr)
                  slots = [load_piece(l, "v", 0), load_piece(l, "v", 1)]
                  for seg, xv, xkey in segs:
                      for i in range(seg.NT):
                          if seg.R == 128 and i == seg.NT - 2 and pending[0] is not None:
                              transposes(segP, l, [pending[0]])
                              pending[0] = None
                          v_piece(seg, l, slots, i, seg.R == 32)
                  _stage(2)
                  def c_xb(seg, eb, bk, bkey):
                      P.op("act", (lambda e: e.activation(
                          out=seg.xbe[:, eb, PST:PST + seg.T], in_=bk[:, 0:seg.T], func=AF.Copy)),
                          reads=(bkey,), writes=(("xbe", seg.name, eb),))
                  fm_piece(l, "xb", segs, c_xb)
                  for seg, xv, xkey in segs:
                      n, T = seg.name, seg.T
                      rd = tuple(("xbe", n, cb) for cb in range(8))
                      if seg.R == 128:
                          if u < NU - 1:
                              P.op("act", (lambda e, seg=seg, l=l, T=T: e.activation(out=hist[:, l], in_=seg.xbe[:, :, T:T + PST], func=AF.Copy)),
                                   reads=rd, writes=(("hist", l),))
                          else:
                              pool_rows_out(seg, l, pp_d, "st_pp")
                      else:
                          pool_rows_out(seg, l, pso_d, "st_ps")
                  pool_thunks = {seg.name: pooling(seg, l, first=(seg.R == 128 and u == 0)) for seg, xv, xkey in segs}
                  _stage(3)
                  def c_za(seg, eb, bk, bkey):
                      P.op("act", (lambda e: e.activation(out=seg.sza[:, eb, :], in_=bk[:, 0:seg.T], func=AF.Silu)),
                           reads=(bkey,), writes=(("sza", seg.name, eb),))
                      spatial_head(seg, l, eb)
                      th = pool_thunks[seg.name]
                      for _ in range(2):
                          if th:
                              th.pop(0)()
                  def c_zb(seg, eb, bk, bkey):
                      P.op("act", (lambda e: e.activation(out=seg.szb[:, eb, :], in_=bk[:, 0:seg.T], func=AF.Silu)),
                           reads=(bkey,), writes=(("szb", seg.name, eb),))
                      pool_linear_blk(seg, l, eb)
                  fm_piece(l, "za", segs, c_za)
                  for seg, xv, xkey in segs:
                      th = pool_thunks[seg.name]
                      while th:
                          th.pop(0)()
                  fm_piece(l, "zb", segs, c_zb)
                  _stage(5)
                  def c_u(seg, eb, bk, bkey):
                      P.op("dve", (lambda e: e.tensor_tensor(
                          out=seg.mixT[:, eb, :], in0=bk[:, 0:seg.T], in1=seg.sza[:, eb, :], op=ALU.mult)),
                          reads=(bkey, ("sza", seg.name, eb)), writes=(("mixT", seg.name, eb),))
                  fm_piece(l, "u", segs, c_u)
                  if l == 0 and u + 1 < NU:
                      load_x(u + 1)
                  _stage(6)
                  slots4 = {}
                  for hf in range(2):
                      slots4[(0, hf)] = load_piece(l, "woa", hf)
                      slots4[(1, hf)] = load_piece(l, "wob", hf)
                  samp_T = False
                  if len(segs) > 1:
                      out_tile(segS, l, xsam[:], ("xs",), slots4, 0)
                      if l + 1 < DEPTH:
                          pre_tile(segS, xsam[:], ("xs",), l + 1, 0, 0)
                          pre_xs(segS, xsam[:], ("xs",), 0, 0)
                          samp_T = True
                      else:
                          final_stats(segS, xsam[:], ("xs",), 0, 2)
                          final_tile(segS, xsam[:], ("xs",), 0, 2)
                          P.dma("pool", "st_ys", lambda e: e.dma_start(out=ys_d, in_=xsam[:, 0, :]),
                                reads=((("xs",), 0),))
                          out_sems.append("st_ys")
                  seg, xv, xkey = segs[0]
                  if l + 1 < DEPTH:
                      nxt = (xv, xkey, l + 1)
                  elif u + 1 < NU:
                      nxt = (xbuf[:, (u + 1) % 2], ("x", (u + 1) % 2), 0)
                  else:
                      nxt = None
                  for i in range(4):
                      out_tile(seg, l, xv, xkey, slots4, i)
                      if l + 1 == DEPTH:
                          final_stats(seg, xv, xkey, i, 8 + i)
                          final_tile(seg, xv, xkey, i, 8 + i)
                      if nxt is not None:
                          pre_tile(seg, nxt[0], nxt[1], nxt[2], i, i)
                          pre_xs(seg, nxt[0], nxt[1], i, i)
                          if i >= 1:
                              transposes(seg, nxt[2], [i - 1])
                      if i == 0 and samp_T:
                          transposes(segS, l + 1, [0])
                  if nxt is not None:
                      pending[0] = 3
                  if l + 1 == DEPTH:
                      P.dma("pool", ("st_y", slot), (lambda e, slot=slot, u=u: e.dma_start(
                          out=yp_d[u * 512:(u + 1) * 512, :].rearrange("(a p) d -> p a d", p=128), in_=xbuf[:, slot])),
                          reads=tuple((("x", slot), i) for i in range(4)),
                          writes=tuple((("x", slot), i) for i in range(4)))
                      if ("st_y", slot) not in out_sems:
                          out_sems.append(("st_y", slot))
        try:
            main_loop()
        except _Stop:
            pass
        if "waitall" in DBG:
            P.wait_all("sp", [k for k in P.cnt if k not in ENGS])
        P.wait_all("sp", out_sems)
        P.emit(st)
    return nc


_CACHE = {}


def _host_inputs(x_prompt, x_sample, state_pool, norm_g, w_in, ln_v_g, ln_v_b, w_spatial, b_spatial,
                 w_pool, pool_scale, w_out, final_g, nb):
    f = lambda a: np.ascontiguousarray(np.asarray(a, dtype=np.float32))
    pp = lambda a: f(np.asarray(a, np.float32).reshape(DEPTH, 8, 128).transpose(2, 0, 1))
    shared = {
        "w_in": f(w_in), "w_out": f(w_out),
        "wsT": f(np.asarray(w_spatial, np.float32).transpose(3, 0, 1, 2)),
        "wpool": f(w_pool),
        "ng": pp(norm_g), "psc": pp(pool_scale), "lng": pp(ln_v_g), "lnb": pp(ln_v_b),
        "lngr": f(ln_v_g), "lnbr": f(ln_v_b),
        "bsp": f(np.asarray(b_spatial, np.float32).reshape(-1)), "fg": f(final_g),
        "ident": np.eye(128, dtype=np.float32), "identf": np.eye(128, dtype=np.float32),
        "invc": np.ascontiguousarray(np.broadcast_to(
            np.array([[1.0 / min(t + 1, w) for t in range(PST)] for w in WINS], np.float32)[None], (128, 4, PST))),
    }
    xp = np.asarray(x_prompt, np.float32)
    xs = np.asarray(x_sample, np.float32)
    spool = np.asarray(state_pool, np.float32)
    maps = []
    for b in range(nb):
        m = dict(shared)
        m["xp"] = f(xp[b])
        m["xs"] = f(xs[b])
        m["sp"] = f(spool[:, b].reshape(DEPTH, PST, 8, 128).transpose(3, 0, 2, 1))
        maps.append(m)
    return maps


def kernel(x_prompt, x_sample, state_pool, norm_g, w_in, ln_v_g, ln_v_b, w_spatial, b_spatial,
           w_pool, pool_scale, w_out, final_g):
    nb = int(np.asarray(x_prompt).shape[0])
    SEQ = int(np.asarray(x_prompt).shape[1])
    if SEQ not in _CACHE:
        _CACHE[SEQ] = build_program(SEQ)
    nc = _CACHE[SEQ]
    maps = _host_inputs(x_prompt, x_sample, state_pool, norm_g, w_in, ln_v_g, ln_v_b, w_spatial, b_spatial,
                        w_pool, pool_scale, w_out, final_g, nb)
    res = run_bass_kernel_spmd(nc, maps, core_ids=list(range(nb)))
    r = res.results
    y_prompt = np.stack([np.asarray(r[b]["yp"], np.float32) for b in range(nb)], 0)
    y_sample = np.stack([np.asarray(r[b]["ys"], np.float32) for b in range(nb)], 0)
    pool_p = np.stack([np.asarray(r[b]["pp"], np.float32) for b in range(nb)], 1)
    pool_s = np.stack([np.asarray(r[b]["pso"], np.float32) for b in range(nb)], 1)
    v_s = np.stack([np.asarray(r[b]["vs"], np.float32) for b in range(nb)], 1)
    return (y_prompt, y_sample, pool_p, pool_s, v_s)
```

```python
import contextlib
import os
DBG = os.environ.get('KDBG', '')
import numpy as np
import concourse.bass as bass
import concourse.mybir as mybir
from concourse.bass_utils import run_bass_kernel_spmd

F32 = mybir.dt.float32
BF16 = mybir.dt.bfloat16
ALU = mybir.AluOpType
AF = mybir.ActivationFunctionType

ENGS = ("pe", "act", "dve", "pool", "sp")
D = 1024
DEPTH = 2
NH = 8
PST = 15
EPS = 1e-6
WINS = (2, 4, 8, 16)
WSLOTS = 6


class Prog:
    def __init__(self, nc):
        self.nc = nc
        self.ops = {e: [] for e in ENGS}
        self.cnt = {}
        self.known = {e: {} for e in ENGS}
        self.res = {}

    def _collect(self, reads, writes):
        deps = {}

        def add(sig):
            if sig is None:
                return
            k, v = sig
            if deps.get(k, 0) < v:
                deps[k] = v

        for r in reads:
            st = self.res.get(r)
            if st is not None:
                add(st[0])
        for w in writes:
            st = self.res.get(w)
            if st is not None:
                add(st[0])
                for k, v in st[1].items():
                    add((k, v))
        return deps

    def _record(self, sig, reads, writes):
        for r in reads:
            st = self.res.setdefault(r, [None, {}])
            if st[1].get(sig[0], 0) < sig[1]:
                st[1][sig[0]] = sig[1]
        for w in writes:
            self.res[w] = [sig, {}]

    def _waits(self, eng, deps):
        out = []
        kn = self.known[eng]
        for k, v in deps.items():
            if kn.get(k, 0) < v:
                kn[k] = v
                out.append((k, v))
        return out

    def op(self, eng, fn, reads=(), writes=()):
        deps = self._collect(reads, writes)
        waits = self._waits(eng, deps)
        self.cnt[eng] = self.cnt.get(eng, 0) + 1
        sig = (eng, self.cnt[eng])
        self.ops[eng].append((waits, fn, (eng, 1)))
        self._record(sig, reads, writes)
        return sig

    def dma(self, eng, semkey, fn, reads=(), writes=()):
        deps = self._collect(reads, writes)
        waits = self._waits(eng, deps)
        self.cnt[semkey] = self.cnt.get(semkey, 0) + 16
        sig = (semkey, self.cnt[semkey])
        self.ops[eng].append((waits, fn, (semkey, 16)))
        self._record(sig, reads, writes)
        return sig

    def dma_group(self, eng, semkey, items):
        final = self.cnt.get(semkey, 0) + 16 * len(items)
        for fn, reads, writes in items:
            deps = self._collect(reads, writes)
            waits = self._waits(eng, deps)
            self.ops[eng].append((waits, fn, (semkey, 16)))
        self.cnt[semkey] = final
        for fn, reads, writes in items:
            self._record((semkey, final), reads, writes)

    def pe_group(self, mms, writes):
        eng = "pe"
        sig = (eng, self.cnt.get(eng, 0) + 1)
        n = len(mms)
        for i, (fn, reads) in enumerate(mms):
            deps = self._collect(reads, writes if i == 0 else ())
            waits = self._waits(eng, deps)
            self.ops[eng].append((waits, fn, (eng, 1) if i == n - 1 else None))
        self.cnt[eng] = sig[1]
        allreads = set()
        for _, reads in mms:
            allreads.update(reads)
        self._record(sig, tuple(allreads), writes)
        return sig

    def wait_all(self, eng, semkeys):
        deps = {k: self.cnt[k] for k in semkeys if self.cnt.get(k, 0) > 0}
        waits = self._waits(eng, deps)
        if waits:
            self.ops[eng].append((waits, None, None))

    def emit(self, stack):
        nc = self.nc
        sems = {}
        for k in self.cnt:
            sems[k] = stack.enter_context(nc.semaphore("s_" + str(k)))
        block = stack.enter_context(nc.Block())
        ops = self.ops

        def run(engine, lst):
            for waits, fn, inc in lst:
                for k, v in waits:
                    engine.wait_ge(sems[k], v)
                if fn is None:
                    continue
                ins = fn(engine)
                if inc is not None:
                    ins.then_inc(sems[inc[0]], inc[1])

        @block.tensor
        def _(e):
            run(e, ops["pe"])

        @block.scalar
        def _(e):
            run(e, ops["act"])

        @block.vector
        def _(e):
            run(e, ops["dve"])

        @block.gpsimd
        def _(e):
            run(e, ops["pool"])

        @block.sync
        def _(e):
            run(e, ops["sp"])


class Seg:
    pass


class _Stop(Exception):
    pass


def _stage(n):
    if ("stage=%d," % n) in DBG + ",":
        raise _Stop()


PIECES = ("v", "xb", "za", "zb", "u", "woa", "wob")
IN_COL = {"u": 0, "v": 1024, "za": 2048, "xb": 3072, "zb": 4096}


def build_program(SEQ):
    NU = SEQ // 512
    nc = bass.Bass("TRN2", target_bir_lowering=False)
    if "noprecook" in DBG:
        nc.dge_precook = False
    dt_in = lambda name, shape: nc.dram_tensor(name, shape, F32, kind="ExternalInput").ap()
    dt_out = lambda name, shape: nc.dram_tensor(name, shape, F32, kind="ExternalOutput").ap()
    xp_d = dt_in("xp", [SEQ, D])
    xs_d = dt_in("xs", [32, D])
    sp_d = dt_in("sp", [128, DEPTH, 8, PST])
    identf_d = dt_in("identf", [128, 128])
    w_in_d = dt_in("w_in", [DEPTH, D, 5 * D])
    w_out_d = dt_in("w_out", [DEPTH, 2 * D, D])
    wsT_d = dt_in("wsT", [128, DEPTH, NH, 128])
    wpool_d = dt_in("wpool", [DEPTH, 4, 256, 256])
    ng_d = dt_in("ng", [128, DEPTH, 8])
    psc_d = dt_in("psc", [128, DEPTH, 8])
    lng_d = dt_in("lng", [128, DEPTH, 8])
    lnb_d = dt_in("lnb", [128, DEPTH, 8])
    lngr_d = dt_in("lngr", [DEPTH, D])
    lnbr_d = dt_in("lnbr", [DEPTH, D])
    bsp_d = dt_in("bsp", [DEPTH * NH * 128])
    fg_d = dt_in("fg", [D])
    ident_d = dt_in("ident", [128, 128])
    invc_d = dt_in("invc", [128, 4, PST])
    yp_d = dt_out("yp", [SEQ, D])
    ys_d = dt_out("ys", [32, D])
    pp_d = dt_out("pp", [DEPTH, PST, D])
    pso_d = dt_out("pso", [DEPTH, PST, D])
    vs_d = dt_out("vs", [DEPTH, 32, D])
    scr = {}
    for l in range(DEPTH):
        for pc in PIECES:
            for hf in range(2):
                scr[(l, pc, hf)] = nc.dram_tensor("scr_%d_%s_%d" % (l, pc, hf), [128, 8, 512], BF16).ap()

    st = contextlib.ExitStack()
    with st:
        sb = lambda name, shape, dt: st.enter_context(nc.sbuf_tensor("sb_" + name, shape, dt))
        ps = st.enter_context(nc.psum_tensor("ps", [128, 4096], F32))
        wring = sb("wring", [128, WSLOTS, 8, 512], BF16)
        wsT = sb("wsT_b", [128, DEPTH, NH, 128], BF16)
        wpool = sb("wpool_b", [128, DEPTH, 4, 2, 256], BF16)
        ng = sb("ng", [128, DEPTH, 8], F32)
        psc = sb("psc", [128, DEPTH, 8], F32)
        lng = sb("lng", [128, DEPTH, 8], F32)
        lnb = sb("lnb", [128, DEPTH, 8], F32)
        biasf = sb("biasf", [128, DEPTH, NH, 128], F32)
        fgB = sb("fgB", [128, D], F32)
        ident = sb("ident_b", [128, 128], BF16)
        ones = sb("ones_b", [128, 128], BF16)
        invc = sb("invc", [128, 4, PST], F32)
        hist = sb("hist", [128, DEPTH, 8, PST], F32)
        hist_s = sb("hist_s", [128, DEPTH, 8, PST], F32)
        identf = sb("identf", [128, 128], F32)
        xbuf = sb("xbuf", [128, 2, 4, D], F32)
        xsam = sb("xsam", [32, 1, D], F32)

        def mkseg(name, T, R, NT):
            s = Seg()
            s.name, s.T, s.R, s.NT = name, T, R, NT
            s.L = PST + T
            s.xsvn = sb(name + "_xsvn", [R, NT, D], BF16)
            s.hT = sb(name + "_hT", [128, 8, T], BF16)
            s.xbe = sb(name + "_xbe", [128, 8, s.L], F32)
            s.pa = sb(name + "_pa", [128, 2, s.L], F32)
            s.pb = sb(name + "_pb", [128, 2, s.L], F32)
            s.dT = sb(name + "_dT", [128, 8, T], BF16)
            s.sza = sb(name + "_sza", [128, 8, T], BF16)
            s.szb = sb(name + "_szb", [128, 8, T], BF16)
            s.t1 = sb(name + "_t1", [128, 2, T], F32)
            s.mixT = sb(name + "_mixT", [128, 16, T], BF16)
            s.ms = sb(name + "_ms", [R, 3 * NT], F32)
            s.rt = sb(name + "_rt", [R, 3 * NT], F32)
            s.rstd = sb(name + "_rstd", [R, 3 * NT], F32)
            s.bst = sb(name + "_bst", [R, NT, 12], F32)
            s.mv = sb(name + "_mv", [R, NT, 2], F32)
            s.fix = sb(name + "_fix", [128, PST], F32)
            return s

        segP = mkseg("P", 512, 128, 4)
        segS = mkseg("S", 32, 32, 1)
        vnf = sb("S_vnf", [32, 512], F32)

        P = Prog(nc)
        nbank = [0]

        def newbank():
            b = nbank[0] % 8
            nbank[0] += 1
            return b, ps[:, b * 512:(b + 1) * 512], ("bank", b)

        cast_idx = [0]

        def cast_piece(l, pc):
            slot_key = ("castslot", cast_idx[0] % 2)
            cast_idx[0] += 1
            items = []
            for hf in range(2):
                if pc in IN_COL:
                    c0 = IN_COL[pc] + hf * 512
                    src = w_in_d[l, :, c0:c0 + 512].rearrange("(c p) e -> p c e", p=128)
                else:
                    r0 = 0 if pc == "woa" else 1024
                    src = w_out_d[l, r0:r0 + 1024, hf * 512:(hf + 1) * 512].rearrange("(c p) e -> p c e", p=128)
                dst = scr[(l, pc, hf)]
                items.append(((lambda e, dst=dst, src=src: e.dma_start(out=dst, in_=src)), (), (("scr", l, pc, hf), slot_key)))
            P.dma_group("pool", ("cast", l, pc), items)
        P.dma_group("pool", "c_cast", [
            (lambda e: e.dma_start(out=ident[:], in_=ident_d), (), ("ident",)),
            (lambda e: e.dma_start(out=wsT[:], in_=wsT_d), (), ("wsT",)),
            (lambda e: e.dma_start(out=wpool[:, 0], in_=wpool_d[0].rearrange("g (c p) d -> p g c d", p=128)), (), (("wpool", 0),)),
            (lambda e: e.dma_start(out=wpool[:, 1], in_=wpool_d[1].rearrange("g (c p) d -> p g c d", p=128)), (), (("wpool", 1),)),
        ])
        for l in range(DEPTH):
            for pc in PIECES:
                if "nocast" not in DBG:
                    cast_piece(l, pc)
        P.dma_group("sp", "c_ws", [
            (lambda e: e.dma_start(out=ng[:], in_=ng_d), (), ("ng",)),
            (lambda e: e.dma_start(out=psc[:], in_=psc_d), (), ("psc",)),
            (lambda e: e.dma_start(out=lng[:], in_=lng_d), (), ("lng",)),
            (lambda e: e.dma_start(out=lnb[:], in_=lnb_d), (), ("lnb",)),
            (lambda e: e.dma_start(out=invc[:], in_=invc_d), (), ("invc",)),
            (lambda e: e.dma_start(out=fgB[:], in_=fg_d.partition_broadcast(128)), (), ("fgB",)),
            (lambda e: e.dma_start(out=xsam[:, 0, :], in_=xs_d), (), ((("xs",), 0),)),
            (lambda e: e.dma_start(out=hist_s[:], in_=sp_d), (), ("hist_s",)),
            (lambda e: e.dma_start(out=identf[:], in_=identf_d), (), ("identf",)),
        ])
        if "stop0" in DBG:
            P.wait_all("sp", [k for k in P.cnt if k not in ENGS]); P.emit(st); return nc
        P.op("dve", lambda e: e.memset(ones[:], 1.0), writes=("ones",))
        P.op("dve", lambda e: e.memset(wsT[64:128, :, :, 0:64], 0.0), reads=(), writes=("wsT",))
        bspB = segP.t1[:].rearrange("p a t -> p (a t)")
        t1keys = (("t1", "P", 0), ("t1", "P", 1))
        for l in range(DEPTH):
            P.dma("sp", "c_bsp", (lambda e, l=l: e.dma_start(
                out=bspB, in_=bsp_d[l * NH * 128:(l + 1) * NH * 128].partition_broadcast(128))), writes=t1keys)
            for hh in range(2):
                b, bk, bkey = newbank()
                P.pe_group([((lambda e, bk=bk, l=l, hh=hh: e.matmul(
                    bk, lhsT=ones[:], rhs=wsT[:, l, hh * 4:(hh + 1) * 4, :], start=True, stop=True)),
                    ("ones", "wsT"))], writes=(bkey,))
                for h4 in range(4):
                    h = hh * 4 + h4
                    o0 = h * 128
                    P.op("dve", (lambda e, bk=bk, l=l, h=h, h4=h4, o0=o0: e.scalar_tensor_tensor(
                        out=biasf[:, l, h, :], in0=bk[:, h4 * 128:(h4 + 1) * 128], scalar=lnb[:, l, h:h + 1],
                        in1=bspB[:, o0:o0 + 128], op0=ALU.mult, op1=ALU.add)),
                        reads=(bkey, "lnb") + t1keys, writes=(("biasf", l, h),))

        wcount = [0]

        def load_piece(l, pc, hf):
            k = wcount[0] % WSLOTS
            wcount[0] += 1
            src = scr[(l, pc, hf)]
            P.dma("sp", ("wl", k), (lambda e, k=k, src=src: e.dma_start(out=wring[:, k], in_=src)),
                  reads=(("scr", l, pc, hf),), writes=(("wslot", k),))
            return k

        def xview(seg, slot):
            return xbuf[:, slot] if seg.R == 128 else xsam[:]

        def pre_tile(seg, xv, xkey, l, i, col):
            R = seg.R
            n = seg.name
            P.op("act", (lambda e: e.activation(out=seg.xsvn[:, i, :], in_=xv[:, i, :], func=AF.Square,
                                                scale=1.0 / 32.0, accum_out=seg.ms[:, col:col + 1])),
                 reads=((xkey, i),), writes=(("xsvn", n, i), ("ms", n, col)))
            P.op("act", (lambda e: e.activation(out=seg.rt[:, col:col + 1], in_=seg.ms[:, col:col + 1], func=AF.Sqrt,
                                                bias=EPS, scale=1.0)),
                 reads=(("ms", n, col),), writes=(("rt", n, col),))
            P.op("dve", (lambda e: e.reciprocal(out=seg.rstd[:, col:col + 1], in_=seg.rt[:, col:col + 1])),
                 reads=(("rt", n, col),), writes=(("rstd", n, col),))

        def pre_xs(seg, xv, xkey, i, col):
            n = seg.name
            P.op("act", (lambda e: e.activation(out=seg.xsvn[:, i, :], in_=xv[:, i, :], func=AF.Identity,
                                                scale=seg.rstd[:, col:col + 1])),
                 reads=((xkey, i), ("rstd", n, col)), writes=(("xsvn", n, i),))

        def transposes(seg, l, tiles, evac_i=[0]):
            R, n = seg.R, seg.name
            nt = len(tiles)
            W = nt * 128 if R == 128 else 32
            nj = max(1, min(8, 1024 // W))
            for j0 in range(0, 8, nj):
                b, bk, bkey = newbank()
                psb = bk.bitcast(BF16)
                mms = []
                for jj in range(nj):
                    j = j0 + jj
                    for ti, i in enumerate(tiles):
                        c0 = jj * W + ti * 128
                        mms.append(((lambda e, psb=psb, c0=c0, i=i, j=j: e.transpose(
                            out=psb[:, c0:c0 + R], in_=seg.xsvn[:, i, j * 128:(j + 1) * 128], identity=ident[0:R, 0:R])),
                            (("xsvn", n, i), "ident")))
                P.pe_group(mms, writes=(bkey,))
                for jj in range(nj):
                    j = j0 + jj
                    t0 = tiles[0] * 128
                    src = psb[:, jj * W:(jj + 1) * W]
                    dst = seg.hT[:, j, t0:t0 + W]
                    wr = tuple(("hT", n, j, i) for i in tiles)
                    if True:
                        P.op("dve", (lambda e, src=src, dst=dst, j=j: e.tensor_scalar(
                            out=dst, in0=src, scalar1=ng[:, l, j:j + 1], scalar2=None, op0=ALU.mult)),
                            reads=(bkey, "ng"), writes=wr)
                    else:
                        P.op("act", (lambda e, src=src, dst=dst, j=j: e.activation(
                            out=dst, in_=src, func=AF.Identity, scale=ng[:, l, j:j + 1])),
                            reads=(bkey, "ng"), writes=wr)
                    evac_i[0] += 1

        def hT_reads(seg, dc):
            return tuple(("hT", seg.name, dc, i) for i in range(seg.NT))

        def v_piece(seg, l, slots, i, sample_out):
            R, n = seg.R, seg.name
            banks = []
            for hf in range(2):
                b, bk, bkey = newbank()
                k = slots[hf]
                mms = []
                for dc in range(8):
                    mms.append(((lambda e, bk=bk, dc=dc, k=k: e.matmul(
                        bk[0:R, :], lhsT=seg.hT[:, dc, i * 128:i * 128 + R], rhs=wring[:, k, dc, :],
                        start=(dc == 0), stop=(dc == 7))),
                        (("hT", n, dc, i), ("wslot", k))))
                P.pe_group(mms, writes=(bkey,))
                banks.append((bk, bkey))
                P.op("dve", (lambda e, bk=bk, hf=hf: e.bn_stats(out=seg.bst[:, i, hf * 6:(hf + 1) * 6], in_=bk[0:R, :])),
                     reads=(bkey,), writes=(("bst", n, i, hf),))
            P.op("dve", (lambda e: e.bn_aggr(out=seg.mv[:, i, :], in_=seg.bst[:, i, :])),
                 reads=(("bst", n, i, 0), ("bst", n, i, 1)), writes=(("mv", n, i),))
            c = seg.NT + i
            P.op("act", (lambda e: e.activation(out=seg.rt[:, c:c + 1], in_=seg.mv[:, i, 1:2], func=AF.Sqrt, bias=EPS, scale=1.0)),
                 reads=(("mv", n, i),), writes=(("rt", n, c),))
            P.op("dve", (lambda e: e.reciprocal(out=seg.rstd[:, c:c + 1], in_=seg.rt[:, c:c + 1])),
                 reads=(("rt", n, c),), writes=(("rstd", n, c),))
            if sample_out:
                P.dma("pool", "s_lg", lambda e: e.dma_start(out=segP.pa[0:32, :, 0:512], in_=lngr_d[l].rearrange("(a c) -> a c", a=2).partition_broadcast(32)),
                      writes=(("pa", "P"),))
                P.dma("pool", "s_lb", lambda e: e.dma_start(out=segP.pb[0:32, :, 0:512], in_=lnbr_d[l].rearrange("(a c) -> a c", a=2).partition_broadcast(32)),
                      writes=(("pb", "P"),))
            for hf in range(2):
                bk, bkey = banks[hf]
                if sample_out:
                    P.op("dve", (lambda e, bk=bk, hf=hf: e.tensor_scalar(
                        out=vnf[:], in0=bk[0:R, :], scalar1=seg.mv[:, i, 0:1],
                        scalar2=seg.rstd[:, c:c + 1], op0=ALU.subtract, op1=ALU.mult)),
                        reads=(bkey, ("mv", n, i), ("rstd", n, c)), writes=("vnf",))
                    P.op("dve", (lambda e, hf=hf: e.tensor_tensor(out=vnf[:], in0=vnf[:], in1=segP.pa[0:32, hf, 0:512], op=ALU.mult)),
                         reads=("vnf", ("pa", "P")), writes=("vnf",))
                    P.op("dve", (lambda e, hf=hf: e.tensor_tensor(out=vnf[:], in0=vnf[:], in1=segP.pb[0:32, hf, 0:512], op=ALU.add)),
                         reads=("vnf", ("pb", "P")), writes=("vnf",))
                    P.dma("pool", "st_vs", (lambda e, hf=hf: e.dma_start(out=vs_d[l, :, hf * 512:(hf + 1) * 512], in_=vnf[:])),
                          reads=("vnf",))
                P.op("dve", (lambda e, bk=bk, hf=hf: e.tensor_scalar(
                    out=seg.xsvn[:, i, hf * 512:(hf + 1) * 512], in0=bk[0:R, :], scalar1=seg.mv[:, i, 0:1],
                    scalar2=seg.rstd[:, c:c + 1], op0=ALU.subtract, op1=ALU.mult)),
                    reads=(bkey, ("mv", n, i), ("rstd", n, c)), writes=(("xsvn", n, i),))

        def fm_group(seg, k, eo):
            n, T = seg.name, seg.T
            b, bk, bkey = newbank()
            mms = []
            for dc in range(8):
                mms.append(((lambda e, bk=bk, dc=dc: e.matmul(
                    bk[:, 0:T], lhsT=wring[:, k, dc, eo * 128:(eo + 1) * 128], rhs=seg.hT[:, dc, 0:T],
                    start=(dc == 0), stop=(dc == 7))),
                    hT_reads(seg, dc) + (("wslot", k),)))
            P.pe_group(mms, writes=(bkey,))
            return bk, bkey

        def pooling(seg, l, first):
            n, T, L = seg.name, seg.T, seg.L
            thunks = []

            def emit(*a, **k):
                thunks.append(lambda: P.op(*a, **k))
            for g in range(4):
                w = WINS[g]
                src = seg.xbe[:, 2 * g:2 * g + 2, :]
                skeys = (("xbe", n, 2 * g), ("xbe", n, 2 * g + 1))
                bufs = [(seg.pa, ("pa", n)), (seg.pb, ("pb", n))]
                cur, ckeys = src, skeys
                sh = 1
                for lev in range(g + 1):
                    dstb, dkey = bufs[lev % 2]
                    lo = 2 * sh - 1
                    emit("dve", (lambda e, dstb=dstb, cur=cur, lo=lo, sh=sh: e.tensor_tensor(
                        out=dstb[:, :, lo:L], in0=cur[:, :, lo:L], in1=cur[:, :, lo - sh:L - sh], op=ALU.add)),
                        reads=ckeys, writes=(dkey,))
                    cur, ckeys = dstb, (dkey,)
                    sh *= 2
                emit("dve", (lambda e, cur=cur, src=src, g=g, w=w: e.scalar_tensor_tensor(
                    out=seg.dT[:, 2 * g:2 * g + 2, :], in0=cur[:, :, PST:L], scalar=1.0 / w, in1=src[:, :, PST:L],
                    op0=ALU.mult, op1=ALU.subtract)),
                    reads=ckeys + skeys, writes=(("dT", n, 2 * g), ("dT", n, 2 * g + 1)))
                if first:
                    for cb in (2 * g, 2 * g + 1):
                        c2 = cb - 2 * g
                        emit("dve", (lambda e, cur=cur, c2=c2, g=g: e.tensor_tensor(
                            out=seg.fix[:], in0=cur[:, c2, PST:2 * PST], in1=invc[:, g, :], op=ALU.mult)),
                            reads=ckeys + ("invc",), writes=(("fix", n),))
                        emit("dve", (lambda e, cb=cb: e.tensor_tensor(
                            out=seg.dT[:, cb, 0:PST], in0=seg.fix[:], in1=seg.xbe[:, cb, PST:2 * PST], op=ALU.subtract)),
                            reads=(("fix", n), ("xbe", n, cb)), writes=(("dT", n, cb),))
            return thunks

        def spatial_head(seg, l, h):
            n, T, R, NT = seg.name, seg.T, seg.R, seg.NT
            if True:
                b, bk, bkey = newbank()
                mms = []
                for i in range(NT):
                    mms.append(((lambda e, bk=bk, i=i, h=h: e.matmul(
                        bk[:, i * 128:i * 128 + R], lhsT=seg.xsvn[:, i, h * 128:(h + 1) * 128], rhs=wsT[0:R, l, h, 0:R],
                        start=True, stop=True)),
                        (("xsvn", n, i), "wsT")))
                P.pe_group(mms, writes=(bkey,))
                sl = h % 2
                if R == 128:
                    in0 = bk[:, 0:T].rearrange("p (a i) -> p a i", i=128)
                    o = seg.t1[:, sl, :].rearrange("p (a i) -> p a i", i=128)
                    bf = biasf[:, l, h, :]
                    in1 = bass.AP(bf.tensor, bf.offset, [list(bf.ap[0]), [0, NT], list(bf.ap[-1])])
                else:
                    in0 = bk[:, 0:T]
                    o = seg.t1[:, sl, :]
                    in1 = biasf[:, l, h, 0:R]
                P.op("dve", (lambda e, in0=in0, o=o, in1=in1, h=h: e.scalar_tensor_tensor(
                    out=o, in0=in0, scalar=lng[:, l, h:h + 1], in1=in1, op0=ALU.mult, op1=ALU.add)),
                    reads=(bkey, "lng", ("biasf", l, h)), writes=(("t1", n, sl),))
                P.op("dve", (lambda e, sl=sl, h=h: e.tensor_tensor(
                    out=seg.sza[:, h, :], in0=seg.t1[:, sl, :], in1=seg.sza[:, h, :], op=ALU.mult)),
                    reads=(("t1", n, sl), ("sza", n, h)), writes=(("sza", n, h),))

        def pool_linear_blk(seg, l, fb0):
            n, T = seg.name, seg.T
            for g in (fb0 // 2,):
                for db in (fb0 % 2,):
                    b, bk, bkey = newbank()
                    mms = []
                    for cc in range(2):
                        mms.append(((lambda e, bk=bk, g=g, cc=cc, db=db: e.matmul(
                            bk[:, 0:T], lhsT=wpool[:, l, g, cc, db * 128:(db + 1) * 128], rhs=seg.dT[:, 2 * g + cc, :],
                            start=(cc == 0), stop=(cc == 1))),
                            (("dT", n, 2 * g + cc), ("wpool", l))))
                    P.pe_group(mms, writes=(bkey,))
                    fb = 2 * g + db
                    P.op("dve", (lambda e, bk=bk, fb=fb: e.scalar_tensor_tensor(
                        out=seg.mixT[:, 8 + fb, :], in0=bk[:, 0:T], scalar=psc[:, l, fb:fb + 1], in1=seg.szb[:, fb, :],
                        op0=ALU.mult, op1=ALU.mult)),
                        reads=(bkey, "psc", ("szb", n, fb)), writes=(("mixT", n, 8 + fb),))

        def out_tile(seg, l, xv, xkey, slots4, i):
            n, R = seg.name, seg.R
            for hf in range(2):
                b, bk, bkey = newbank()
                mms = []
                order = list(range(8, 16)) + list(range(8))
                for q, ec in enumerate(order):
                    k = slots4[(0 if ec < 8 else 1, hf)]
                    mms.append(((lambda e, bk=bk, ec=ec, k=k, q=q: e.matmul(
                        bk[0:R, :], lhsT=seg.mixT[:, ec, i * 128:i * 128 + R], rhs=wring[:, k, ec % 8, :],
                        start=(q == 0), stop=(q == 15))),
                        (("mixT", n, ec), ("wslot", k))))
                P.pe_group(mms, writes=(bkey,))
                P.op("dve", (lambda e, bk=bk, hf=hf: e.tensor_tensor(
                    out=xv[:, i, hf * 512:(hf + 1) * 512], in0=bk[0:R, :], in1=xv[:, i, hf * 512:(hf + 1) * 512], op=ALU.add)),
                    reads=(bkey, (xkey, i)), writes=((xkey, i),))

        def final_tile(seg, xv, xkey, i, col):
            n, R = seg.name, seg.R
            P.op("dve", (lambda e: e.scalar_tensor_tensor(
                out=xv[:, i, :], in0=xv[:, i, :], scalar=seg.rstd[:, col:col + 1], in1=fgB[0:R, :],
                op0=ALU.mult, op1=ALU.mult)),
                reads=((xkey, i), ("rstd", n, col), "fgB"), writes=((xkey, i),))

        def load_x(u):
            slot = u % 2
            if "x1s0" in DBG: slot = 0
            if "x1r0" in DBG: u = 0
            P.dma("sp", ("xl", slot), (lambda e: e.dma_start(
                out=xbuf[:, slot], in_=xp_d[u * 512:(u + 1) * 512, :].rearrange("(a p) d -> p a d", p=128))),
                writes=tuple((("x", slot), i) for i in range(4)))

        load_x(0)
        if "extra" in DBG:
            for _ in range(6):
                P.dma("sp", "c_extra", lambda e: e.dma_start(out=ng[:], in_=ng_d), writes=("ng",))
        out_sems = ["st_vs"]

        def stats_junk(seg, i):
            v = seg.dT[:].rearrange("p (a b) t -> p a (b t)", b=2) if seg.R == 128 else None
            return v

        def final_stats(seg, xv, xkey, i, col):
            n = seg.name
            if seg.R == 128:
                junk = seg.dT[:, 2 * i:2 * i + 2, :]
                src = xv[:, i, :].rearrange("p (b t) -> p b t", b=2)
                jk = (("dT", n, 2 * i), ("dT", n, 2 * i + 1))
            else:
                junk = seg.xsvn[:, 0, :]
                src = xv[:, i, :]
                jk = (("xsvn", n, 0),)
            P.op("act", (lambda e: e.activation(out=junk, in_=src, func=AF.Square,
                                                scale=1.0 / 32.0, accum_out=seg.ms[:, col:col + 1])),
                 reads=((xkey, i),), writes=jk + (("ms", n, col),))
            P.op("act", (lambda e: e.activation(out=seg.rt[:, col:col + 1], in_=seg.ms[:, col:col + 1], func=AF.Sqrt,
                                                bias=EPS, scale=1.0)),
                 reads=(("ms", n, col),), writes=(("rt", n, col),))
            P.op("dve", (lambda e: e.reciprocal(out=seg.rstd[:, col:col + 1], in_=seg.rt[:, col:col + 1])),
                 reads=(("rt", n, col),), writes=(("rstd", n, col),))

        def pool_rows_out(seg, l, dst_d, semkey):
            n, T = seg.name, seg.T
            stage = segP.t1[:].rearrange("p a t -> p (a t)")
            for hb in range(2):
                b, bk, bkey = newbank()
                mms = []
                for c4 in range(4):
                    cb = hb * 4 + c4
                    mms.append(((lambda e, bk=bk, c4=c4, cb=cb: e.transpose(
                        out=bk[0:PST, c4 * 128:(c4 + 1) * 128], in_=seg.xbe[:, cb, T:T + PST], identity=identf[:])),
                        (("xbe", n, cb), "identf")))
                P.pe_group(mms, writes=(bkey,))
                P.op("act", (lambda e, bk=bk, hb=hb: e.activation(
                    out=stage[0:PST, hb * 512:(hb + 1) * 512], in_=bk[0:PST, :], func=AF.Copy)),
                    reads=(bkey,), writes=(("t1", "P", hb),))
            P.dma("pool", semkey, (lambda e: e.dma_start(out=dst_d[l], in_=stage[0:PST, :])),
                  reads=(("t1", "P", 0), ("t1", "P", 1)))
            if semkey not in out_sems:
                out_sems.append(semkey)

        def fm_piece(l, pc, segs, consume):
            for hf in range(2):
                k = load_piece(l, pc, hf)
                for seg, xv, xkey in segs:
                    for eo in range(4):
                        bk, bkey = fm_group(seg, k, eo)
                        consume(seg, hf * 4 + eo, bk, bkey)

        def main_loop():
          pre_tile(segS, xsam[:], ("xs",), 0, 0, 0)
          pre_xs(segS, xsam[:], ("xs",), 0, 0)
          transposes(segS, 0, [0])
          for i in range(4):
              pre_tile(segP, xbuf[:, 0], ("x", 0), 0, i, i)
              pre_xs(segP, xbuf[:, 0], ("x", 0), i, i)
          transposes(segP, 0, [0, 1, 2, 3])
          pending = [None]
          for u in range(NU if "stop" not in DBG else 0):
              slot = u % 2
              segs = [(segP, xbuf[:, slot], ("x", slot))]
              if u == NU - 1:
                  segs.append((segS, xsam[:], ("xs",)))
              for l in range(DEPTH):
                  _stage(1)
                  for seg, xv, xkey in segs:
                      n = seg.name
                      wr = tuple(("xbe", n, cb) for cb in range(8))
                      if seg.R == 128:
                          if u == 0:
                              P.op("dve", (lambda e, seg=seg: e.memset(seg.xbe[:, :, 0:PST], 0.0)), writes=wr)
                          else:
                              P.op("act", (lambda e, seg=seg, l=l: e.activation(out=seg.xbe[:, :, 0:PST], in_=hist[:, l], func=AF.Copy)),
                                   reads=(("hist", l),), writes=wr)
                      else:
                          P.op("act", (lambda e, seg=seg, l=l: e.activation(out=seg.xbe[:, :, 0:PST], in_=hist_s[:, l], func=AF.Copy)),
                               reads=("hist_s",), writes=wr)
                  slots = [load_piece(l, "v", 0), load_piece(l, "v", 1)]
                  for seg, xv, xkey in segs:
                      for i in range(seg.NT):
                          if seg.R == 128 and i == seg.NT - 2 and pending[0] is not None:
                              transposes(segP, l, [pending[0]])
                              pending[0] = None
                          v_piece(seg, l, slots, i, seg.R == 32)
                  _stage(2)
                  def c_xb(seg, eb, bk, bkey):
                      P.op("act", (lambda e: e.activation(
                          out=seg.xbe[:, eb, PST:PST + seg.T], in_=bk[:, 0:seg.T], func=AF.Copy)),
                          reads=(bkey,), writes=(("xbe", seg.name, eb),))
                  fm_piece(l, "xb", segs, c_xb)
                  for seg, xv, xkey in segs:
                      n, T = seg.name, seg.T
                      rd = tuple(("xbe", n, cb) for cb in range(8))
                      if seg.R == 128:
                          if u < NU - 1:
                              P.op("act", (lambda e, seg=seg, l=l, T=T: e.activation(out=hist[:, l], in_=seg.xbe[:, :, T:T + PST], func=AF.Copy)),
                                   reads=rd, writes=(("hist", l),))
                          else:
                              pool_rows_out(seg, l, pp_d, "st_pp")
                      else:
                          pool_rows_out(seg, l, pso_d, "st_ps")
                  pool_thunks = {seg.name: pooling(seg, l, first=(seg.R == 128 and u == 0)) for seg, xv, xkey in segs}
                  _stage(3)
                  for seg, xv, xkey in segs:
                      th = pool_thunks[seg.name]
                      while th:
                          th.pop(0)()
                  def c_za(seg, eb, bk, bkey):
                      P.op("act", (lambda e: e.activation(out=seg.sza[:, eb, :], in_=bk[:, 0:seg.T], func=AF.Silu)),
                           reads=(bkey,), writes=(("sza", seg.name, eb),))
                  def c_zb(seg, eb, bk, bkey):
                      P.op("act", (lambda e: e.activation(out=seg.szb[:, eb, :], in_=bk[:, 0:seg.T], func=AF.Silu)),
                           reads=(bkey,), writes=(("szb", seg.name, eb),))
                      spatial_head(seg, l, eb)
                  fm_piece(l, "za", segs, c_za)
                  fm_piece(l, "zb", segs, c_zb)
                  _stage(5)
                  def c_u(seg, eb, bk, bkey):
                      P.op("dve", (lambda e: e.tensor_tensor(
                          out=seg.mixT[:, eb, :], in0=bk[:, 0:seg.T], in1=seg.sza[:, eb, :], op=ALU.mult)),
                          reads=(bkey, ("sza", seg.name, eb)), writes=(("mixT", seg.name, eb),))
                      pool_linear_blk(seg, l, eb)
                  fm_piece(l, "u", segs, c_u)
                  if l == 0 and u + 1 < NU:
                      load_x(u + 1)
                  _stage(6)
                  slots4 = {}
                  for hf in range(2):
                      slots4[(0, hf)] = load_piece(l, "woa", hf)
                      slots4[(1, hf)] = load_piece(l, "wob", hf)
                  samp_T = False
                  if len(segs) > 1:
                      out_tile(segS, l, xsam[:], ("xs",), slots4, 0)
                      if l + 1 < DEPTH:
                          pre_tile(segS, xsam[:], ("xs",), l + 1, 0, 0)
                          pre_xs(segS, xsam[:], ("xs",), 0, 0)
                          samp_T = True
                      else:
                          final_stats(segS, xsam[:], ("xs",), 0, 2)
                          final_tile(segS, xsam[:], ("xs",), 0, 2)
                          P.dma("pool", "st_ys", lambda e: e.dma_start(out=ys_d, in_=xsam[:, 0, :]),
                                reads=((("xs",), 0),))
                          out_sems.append("st_ys")
                  seg, xv, xkey = segs[0]
                  if l + 1 < DEPTH:
                      nxt = (xv, xkey, l + 1)
                  elif u + 1 < NU:
                      nxt = (xbuf[:, (u + 1) % 2], ("x", (u + 1) % 2), 0)
                  else:
                      nxt = None
                  for i in range(4):
                      out_tile(seg, l, xv, xkey, slots4, i)
                      if l + 1 == DEPTH:
                          final_stats(seg, xv, xkey, i, 8 + i)
                          final_tile(seg, xv, xkey, i, 8 + i)
                      if nxt is not None:
                          pre_tile(seg, nxt[0], nxt[1], nxt[2], i, i)
                          pre_xs(seg, nxt[0], nxt[1], i, i)
                          if i >= 1:
                              transposes(seg, nxt[2], [i - 1])
                      if i == 0 and samp_T:
                          transposes(segS, l + 1, [0])
                  if nxt is not None:
                      pending[0] = 3
                  if l + 1 == DEPTH:
                      P.dma("pool", ("st_y", slot), (lambda e, slot=slot, u=u: e.dma_start(
                          out=yp_d[u * 512:(u + 1) * 512, :].rearrange("(a p) d -> p a d", p=128), in_=xbuf[:, slot])),
                          reads=tuple((("x", slot), i) for i in range(4)),
                          writes=tuple((("x", slot), i) for i in range(4)))
                      if ("st_y", slot) not in out_sems:
                          out_sems.append(("st_y", slot))
        try:
            main_loop()
        except _Stop:
            pass
        if "waitall" in DBG:
            P.wait_all("sp", [k for k in P.cnt if k not in ENGS])
        P.wait_all("sp", out_sems)
        P.emit(st)
    return nc


_CACHE = {}


def _host_inputs(x_prompt, x_sample, state_pool, norm_g, w_in, ln_v_g, ln_v_b, w_spatial, b_spatial,
                 w_pool, pool_scale, w_out, final_g, nb):
    f = lambda a: np.ascontiguousarray(np.asarray(a, dtype=np.float32))
    pp = lambda a: f(np.asarray(a, np.float32).reshape(DEPTH, 8, 128).transpose(2, 0, 1))
    shared = {
        "w_in": f(w_in), "w_out": f(w_out),
        "wsT": f(np.asarray(w_spatial, np.float32).transpose(3, 0, 1, 2)),
        "wpool": f(w_pool),
        "ng": pp(norm_g), "psc": pp(pool_scale), "lng": pp(ln_v_g), "lnb": pp(ln_v_b),
        "lngr": f(ln_v_g), "lnbr": f(ln_v_b),
        "bsp": f(np.asarray(b_spatial, np.float32).reshape(-1)), "fg": f(final_g),
        "ident": np.eye(128, dtype=np.float32), "identf": np.eye(128, dtype=np.float32),
        "invc": np.ascontiguousarray(np.broadcast_to(
            np.array([[1.0 / min(t + 1, w) for t in range(PST)] for w in WINS], np.float32)[None], (128, 4, PST))),
    }
    xp = np.asarray(x_prompt, np.float32)
    xs = np.asarray(x_sample, np.float32)
    spool = np.asarray(state_pool, np.float32)
    maps = []
    for b in range(nb):
        m = dict(shared)
        m["xp"] = f(xp[b])
        m["xs"] = f(xs[b])
        m["sp"] = f(spool[:, b].reshape(DEPTH, PST, 8, 128).transpose(3, 0, 2, 1))
        maps.append(m)
    return maps


def kernel(x_prompt, x_sample, state_pool, norm_g, w_in, ln_v_g, ln_v_b, w_spatial, b_spatial,
           w_pool, pool_scale, w_out, final_g):
    nb = int(np.asarray(x_prompt).shape[0])
    SEQ = int(np.asarray(x_prompt).shape[1])
    if SEQ not in _CACHE:
        _CACHE[SEQ] = build_program(SEQ)
    nc = _CACHE[SEQ]
    maps = _host_inputs(x_prompt, x_sample, state_pool, norm_g, w_in, ln_v_g, ln_v_b, w_spatial, b_spatial,
                        w_pool, pool_scale, w_out, final_g, nb)
    res = run_bass_kernel_spmd(nc, maps, core_ids=list(range(nb)))
    r = res.results
    y_prompt = np.stack([np.asarray(r[b]["yp"], np.float32) for b in range(nb)], 0)
    y_sample = np.stack([np.asarray(r[b]["ys"], np.float32) for b in range(nb)], 0)
    pool_p = np.stack([np.asarray(r[b]["pp"], np.float32) for b in range(nb)], 1)
    pool_s = np.stack([np.asarray(r[b]["pso"], np.float32) for b in range(nb)], 1)
    v_s = np.stack([np.asarray(r[b]["vs"], np.float32) for b in range(nb)], 1)
    return (y_prompt, y_sample, pool_p, pool_s, v_s)
```
